# Optimizing a Trainium2 kernel written in Bass

```python
import jax, jax.numpy as jnp
from jax import lax
import numpy as np

D_MODEL = 1024
BATCH = 4
SEQ = 4096
DEPTH = 4

CHUNK = 64
EPS = 1e-6
N_A = DEPTH // 2
N_B = DEPTH - N_A
A_CHUNK = 128
A_DFF = 2 * D_MODEL
A_GROUPS = 8
A_GROUP_DIM = A_DFF // A_GROUPS
B_HEADS = 16
B_HEAD_DIM = D_MODEL // B_HEADS
N_LEFT_CHUNKS = 8
BAND = (N_LEFT_CHUNKS + 1) * CHUNK
MAX_REL = 256
ATTN_SCALE = B_HEAD_DIM ** -0.5
NEG_INF = -1e30
FFN_HIDDEN = -(-8 * D_MODEL // (3 * 256)) * 256

kernel_name = "yoco_gmlp_chunked_relbias_attention_trunk"


def _rms_norm(x, g):
    xf = x.astype(jnp.float32)
    y = xf * lax.rsqrt(jnp.mean(xf * xf, axis=-1, keepdims=True) + EPS)
    return (y * g.astype(jnp.float32)).astype(x.dtype)


def _spatial_mask():
    pos = jnp.arange(A_CHUNK) // CHUNK
    return pos[:, None] >= pos[None, :]


def _gmlp_mixer(x, g_norm, w_in, g_sgu, w_s, b_s, w_out):
    b, s, _ = x.shape
    h = _rms_norm(x, g_norm)
    z = jax.nn.gelu(h @ w_in, approximate=False)
    u, v = jnp.split(z, 2, axis=-1)
    v = _rms_norm(v, g_sgu)
    v = v.reshape(b, s // A_CHUNK, A_CHUNK, A_GROUPS, A_GROUP_DIM)
    w = w_s * _spatial_mask().astype(w_s.dtype)[None]
    v = jnp.einsum('gij,bnjgc->bnigc', w, v) + b_s.T[None, None, :, :, None]
    v = v.reshape(b, s, A_DFF)
    return (u * v) @ w_out


def _swiglu_ffn(x, g_norm, w_gate_up, w_down):
    h = _rms_norm(x, g_norm)
    gate, up = jnp.split(h @ w_gate_up, 2, axis=-1)
    return (jax.nn.silu(gate) * up) @ w_down


def _shared_kv(x, g_norm, w_kv):
    b, s, _ = x.shape
    h = _rms_norm(x, g_norm)
    k, v = jnp.split(h @ w_kv, 2, axis=-1)
    k = k.reshape(b, s, B_HEADS, B_HEAD_DIM)
    v = v.reshape(b, s, B_HEADS, B_HEAD_DIM)
    pad = ((0, 0), (N_LEFT_CHUNKS * CHUNK, 0), (0, 0), (0, 0))
    return jnp.pad(k, pad), jnp.pad(v, pad)


def _rel_index():
    qi = jnp.arange(CHUNK)[:, None]
    kj = jnp.arange(BAND)[None, :] - N_LEFT_CHUNKS * CHUNK
    return jnp.clip(qi - kj, -MAX_REL, MAX_REL) + MAX_REL


def _chunked_relbias_attention(x, g_norm, w_q, rel_table, w_o, k_pad, v_pad):
    b, s, _ = x.shape
    n_chunks = s // CHUNK
    q = (_rms_norm(x, g_norm) @ w_q).reshape(b, n_chunks, CHUNK, B_HEADS, B_HEAD_DIM)
    q = jnp.moveaxis(q, 1, 0)
    bias = rel_table[:, _rel_index()].astype(jnp.float32)
    key_offset = jnp.arange(BAND) - N_LEFT_CHUNKS * CHUNK

    def attend(args):
        c, q_c = args
        k_c = lax.dynamic_slice_in_dim(k_pad, c * CHUNK, BAND, axis=1)
        v_c = lax.dynamic_slice_in_dim(v_pad, c * CHUNK, BAND, axis=1)
        sc = jnp.einsum('bqhd,bkhd->bhqk', q_c, k_c).astype(jnp.float32) * ATTN_SCALE + bias
        valid = (c * CHUNK + key_offset) >= 0
        sc = jnp.where(valid, sc, NEG_INF)
        p = jax.nn.softmax(sc, axis=-1).astype(v_c.dtype)
        return jnp.einsum('bhqk,bkhd->bqhd', p, v_c)

    o = lax.map(attend, (jnp.arange(n_chunks), q))
    o = jnp.moveaxis(o, 0, 1).reshape(b, s, D_MODEL)
    return o @ w_o


def _normal(key, shape, scale):
    return jax.random.normal(key, shape, jnp.float32) * scale


def setup_inputs(seed: int = 0) -> dict:
    key = jax.random.key(seed)
    ks = jax.random.split(key, 18)
    return {
        "x": _normal(ks[0], (BATCH, SEQ, D_MODEL), 1.0),
        "a_norm": 1.0 + _normal(ks[1], (N_A, D_MODEL), 0.02),
        "a_w_in": _normal(ks[2], (N_A, D_MODEL, 2 * A_DFF), D_MODEL ** -0.5),
        "a_sgu_norm": 1.0 + _normal(ks[3], (N_A, A_DFF), 0.02),
        "a_w_spatial": _normal(ks[4], (N_A, A_GROUPS, A_CHUNK, A_CHUNK), A_CHUNK ** -0.5),
        "a_b_spatial": 1.0 + _normal(ks[5], (N_A, A_GROUPS, A_CHUNK), 0.1),
        "a_w_out": _normal(ks[6], (N_A, A_DFF, D_MODEL), A_DFF ** -0.5),
        "kv_norm": 1.0 + _normal(ks[7], (D_MODEL,), 0.02),
        "w_kv": _normal(ks[8], (D_MODEL, 2 * D_MODEL), D_MODEL ** -0.5),
        "b_norm": 1.0 + _normal(ks[9], (N_B, D_MODEL), 0.02),
        "b_w_q": _normal(ks[10], (N_B, D_MODEL, D_MODEL), D_MODEL ** -0.5),
        "b_rel_bias": _normal(ks[11], (N_B, B_HEADS, 2 * MAX_REL + 1), 0.1),
        "b_w_o": _normal(ks[12], (N_B, D_MODEL, D_MODEL), D_MODEL ** -0.5),
        "ffn_norm": 1.0 + _normal(ks[13], (DEPTH, D_MODEL), 0.02),
        "ffn_w_gate_up": _normal(ks[14], (DEPTH, D_MODEL, 2 * FFN_HIDDEN), D_MODEL ** -0.5),
        "ffn_w_down": _normal(ks[15], (DEPTH, FFN_HIDDEN, D_MODEL), FFN_HIDDEN ** -0.5),
        "final_norm": 1.0 + _normal(ks[16], (D_MODEL,), 0.02),
    }


def reference(x, a_norm, a_w_in, a_sgu_norm, a_w_spatial, a_b_spatial, a_w_out,
              kv_norm, w_kv, b_norm, b_w_q, b_rel_bias, b_w_o,
              ffn_norm, ffn_w_gate_up, ffn_w_down, final_norm):
    k_pad = v_pad = None
    for layer in range(DEPTH):
        if layer < N_A:
            i = layer
            x = x + _gmlp_mixer(x, a_norm[i], a_w_in[i], a_sgu_norm[i],
                                a_w_spatial[i], a_b_spatial[i], a_w_out[i])
        else:
            if layer == N_A:
                k_pad, v_pad = _shared_kv(x, kv_norm, w_kv)
            i = layer - N_A
            x = x + _chunked_relbias_attention(x, b_norm[i], b_w_q[i], b_rel_bias[i],
                                               b_w_o[i], k_pad, v_pad)
        x = x + _swiglu_ffn(x, ffn_norm[layer], ffn_w_gate_up[layer], ffn_w_down[layer])
    return _rms_norm(x, final_norm)
```

```python
import os
import numpy as np
from contextlib import ExitStack
import concourse.bass as bass
import concourse.mybir as mybir
from concourse.bass_utils import run_bass_kernel_spmd

F32 = mybir.dt.float32
BF16 = mybir.dt.bfloat16
AF = mybir.ActivationFunctionType
ALU = mybir.AluOpType

D = 1024
KC = 8
SEQ = 4096
BATCH = 4
NCORE = 8
OWN = 2048
HALO = 512
NTOK = OWN + HALO
ADFF = 2048
FH = 2816
FJ = 22
EPS = 1e-6
NHEAD = 16
RING = 3
KDBG = os.environ.get('KDBG', '')

G_A = 0
G_F = 16
G_KV = 48
G_B = 56
G_FIN = 72
G_SGU = 80
G_N = 112

SL_A = 31
SL_KV = 4
SL_B = 23
NSLOT = 2 * SL_A + SL_KV + 2 * SL_B


def _kmaj(w):
    k, n = w.shape
    return np.ascontiguousarray(w.reshape(k // 128, 128, n).transpose(1, 0, 2))


def _slot(a):
    a = a.reshape(128, -1)
    out = np.zeros((128, 4096), np.float32)
    out[:, : a.shape[1]] = a
    return out


def _build_wstream(inp):
    slots = []

    def ffn(l):
        wgu = inp["ffn_w_gate_up"][l]
        wd = _kmaj(inp["ffn_w_down"][l])
        for s in range(11):
            cols = np.concatenate([wgu[:, 2 * s * 128:(2 * s + 2) * 128],
                                   wgu[:, FH + 2 * s * 128: FH + (2 * s + 2) * 128]], axis=1)
            slots.append(_slot(_kmaj(cols)))
        for d in range(8):
            slots.append(_slot(wd[:, :, d * 128:(d + 1) * 128]))

    for i in range(2):
        win = inp["a_w_in"][i]
        for s in range(4):
            slots.append(_slot(_kmaj(win[:, ADFF + s * 512: ADFF + (s + 1) * 512])))
        for s in range(4):
            slots.append(_slot(_kmaj(win[:, s * 512:(s + 1) * 512])))
        wo = _kmaj(inp["a_w_out"][i])
        for s in range(4):
            slots.append(_slot(wo[:, :, s * 256:(s + 1) * 256]))
        ffn(i)
    wkv = inp["w_kv"]
    for s in range(2):
        slots.append(_slot(_kmaj(wkv[:, s * 512:(s + 1) * 512])))
    for s in range(2):
        slots.append(_slot(_kmaj(wkv[:, D + s * 512: D + (s + 1) * 512])))
    for i in range(2):
        wq = inp["b_w_q"][i]
        for s in range(2):
            slots.append(_slot(_kmaj(wq[:, s * 512:(s + 1) * 512])))
        wo = inp["b_w_o"][i]
        for s in range(2):
            slots.append(_slot(_kmaj(wo[:, s * 512:(s + 1) * 512])))
        ffn(2 + i)
    assert len(slots) == NSLOT
    return np.stack(slots, 0)


def _gcols(g):
    return np.ascontiguousarray(g.reshape(-1, 128).T)


def _build_small(inp):
    gains = np.zeros((128, G_N), np.float32)
    for i in range(2):
        gains[:, G_A + 8 * i: G_A + 8 * i + 8] = _gcols(inp["a_norm"][i])
        gains[:, G_B + 8 * i: G_B + 8 * i + 8] = _gcols(inp["b_norm"][i])
        gains[:, G_SGU + 16 * i: G_SGU + 16 * i + 16] = _gcols(inp["a_sgu_norm"][i])
    for i in range(4):
        gains[:, G_F + 8 * i: G_F + 8 * i + 8] = _gcols(inp["ffn_norm"][i])
    gains[:, G_KV: G_KV + 8] = _gcols(inp["kv_norm"])
    gains[:, G_FIN: G_FIN + 8] = _gcols(inp["final_norm"])
    bsr = np.ascontiguousarray(np.broadcast_to(inp["a_b_spatial"][None], (128, 2, 8, 128))).astype(np.float32)
    wsT = np.ascontiguousarray(np.transpose(inp["a_w_spatial"], (3, 0, 1, 2))).astype(np.float32)
    kl = np.arange(128)[:, None]
    qq = np.arange(640)[None, :]
    idx = np.clip(qq - kl, -256, 256) + 256
    bias = np.ascontiguousarray(inp["b_rel_bias"][:, :, idx]).astype(np.float32)
    m = (qq // 64)
    kh = (kl // 64)
    valid = (m >= kh) & (m <= 8 + kh)
    bias = np.where(valid[None, None], bias, np.float32(-30000.0)).astype(np.float32)
    return gains, bsr, wsT, bias


class Buf:
    __slots__ = ("w", "r")

    def __init__(self):
        self.w = None
        self.r = {}


class Eng:
    def __init__(self, nc, es, eng, name, ndma=0):
        self.nc, self.es, self.eng, self.name = nc, es, eng, name
        self.gen = 0
        self.sem = es.enter_context(nc.semaphore(f"s_{name}_0"))
        self.cnt = 0
        self.seen = {}
        self.dsem = [[es.enter_context(nc.semaphore(f"d_{name}_{i}")), 0] for i in range(ndma)]
        self.drr = 0

    def next_event(self):
        if self.cnt >= 24000:
            self.gen += 1
            self.sem = self.es.enter_context(self.nc.semaphore(f"s_{self.name}_{self.gen}"))
            self.cnt = 0
        self.cnt += 1
        return (self.sem, self.cnt)


class Prog:
    def __init__(self, nc, es):
        self.nc = nc
        self.pe = Eng(nc, es, nc.tensor, "pe")
        self.act = Eng(nc, es, nc.scalar, "act", ndma=2)
        self.dve = Eng(nc, es, nc.vector, "dve")
        self.pool = Eng(nc, es, nc.gpsimd, "pool", ndma=RING + 2)
        self.sp = Eng(nc, es, nc.sync, "sp", ndma=8)

    @staticmethod
    def _deps(E, reads, writes):
        need = {}

        def add(ev):
            if ev is None:
                return
            s, v = ev
            if need.get(s, 0) < v:
                need[s] = v

        for b in reads:
            add(b.w)
        for b in writes:
            add(b.w)
            for s, v in b.r.items():
                add((s, v))
        out = []
        for s, v in need.items():
            if E.seen.get(s, 0) >= v:
                continue
            E.seen[s] = v
            out.append((s, v))
        return out

    @staticmethod
    def _mark(ev, reads, writes, acc=()):
        s, v = ev
        for b in writes:
            b.w = ev
            b.r = {}
        for b in acc:
            b.w = ev
            b.r = {}
        for b in reads:
            if b.r.get(s, 0) < v:
                b.r[s] = v

    def op(self, E, fn, reads=(), writes=()):
        deps = self._deps(E, reads, writes)
        for s, v in deps[1:]:
            E.eng.wait_ge(s, v)
        ins = fn(E.eng)
        if deps:
            ins._wait_ge(deps[0][0], deps[0][1])
        ev = E.next_event()
        ins.then_inc(ev[0], 1)
        self._mark(ev, reads, writes)

    def mm(self, fns, reads=(), writes=(), acc=()):
        E = self.pe
        deps = self._deps(E, reads, writes)
        for s, v in deps:
            E.eng.wait_ge(s, v)
        ins = None
        for f in fns:
            ins = f(E.eng)
        ev = E.next_event()
        ins.then_inc(ev[0], 1)
        self._mark(ev, reads, writes, acc)

    def dma(self, Q, out, in_, reads=(), writes=()):
        ds = Q.dsem[Q.drr]
        Q.drr = (Q.drr + 1) % len(Q.dsem)
        sem, val = ds
        if val > 0 and Q.seen.get(sem, 0) < val:
            Q.eng.wait_ge(sem, val)
            Q.seen[sem] = val
        for s, v in self._deps(Q, reads, writes):
            Q.eng.wait_ge(s, v)
        Q.eng.dma_start(out=out, in_=in_).then_inc(sem, 16)
        ds[1] = val + 16
        self._mark((sem, val + 16), reads, writes)


class Region:
    def __init__(self):
        self.pending = {}
        self.cur = []

    def new_bufs(self, n):
        for b in self.cur:
            if b.w is not None:
                s, v = b.w
                if self.pending.get(s, 0) < v:
                    self.pending[s] = v
            for s, v in b.r.items():
                if self.pending.get(s, 0) < v:
                    self.pending[s] = v
        self.cur = []
        out = []
        for _ in range(n):
            b = Buf()
            b.r = dict(self.pending)
            out.append(b)
        self.cur = list(out)
        return out

    def more_bufs(self, n):
        out = []
        for _ in range(n):
            b = Buf()
            b.r = dict(self.pending)
            out.append(b)
        self.cur += out
        return out


def build(nphases=10):
    nc = bass.Bass("TRN2", target_bir_lowering=False)
    x_d = nc.dram_tensor("x_t", [128, KC, NTOK], F32, kind="ExternalInput").ap()
    ws_d = nc.dram_tensor("wstream", [NSLOT, 128, 4096], F32, kind="ExternalInput").ap()
    gains_d = nc.dram_tensor("gains", [128, G_N], F32, kind="ExternalInput").ap()
    bsr_d = nc.dram_tensor("bsr", [128, 2, 8, 128], F32, kind="ExternalInput").ap()
    wsT_d = nc.dram_tensor("wsT", [128, 2, 8, 128], F32, kind="ExternalInput").ap()
    flag_d = nc.dram_tensor("flag", [128, 1], F32, kind="ExternalInput").ap()
    bias_d = nc.dram_tensor("relb", [2, NHEAD, 128, 640], F32, kind="ExternalInput").ap()
    out_d = nc.dram_tensor("out_t", [128, KC, OWN], F32, kind="ExternalOutput").ap()
    kT_d = nc.dram_tensor("kT_scr", [128, KC, NTOK], BF16, kind="Internal").ap()
    v_d = nc.dram_tensor("v_scr", [128, NTOK // 128, 1536], BF16, kind="Internal").ap()

    with ExitStack() as es:
        P = Prog(nc, es)
        PE, ACT, DVE, POOL, SP = P.pe, P.act, P.dve, P.pool, P.sp

        def sb(name, shape, dt):
            return es.enter_context(nc.sbuf_tensor(name, shape, dt))

        xT = sb("xT", [128, KC, 1024], F32)
        hT = sb("hT", [128, KC, 1024], BF16)
        R1 = sb("R1", [128, 22528], BF16)
        R2 = sb("R2", [128, 24576], BF16)
        ring = [sb(f"ring{i}", [128, 4096], BF16) for i in range(RING)]
        ones_bf = sb("ones_bf", [128, 128], BF16)
        gains = sb("gains_sb", [128, G_N], F32)
        bsr = sb("bsr_sb", [128, 2, 8, 128], F32)
        wsT = sb("wsT_sb", [128, 2, 1024], BF16)
        flag = sb("flag_sb", [128, 1], F32)
        epsT = sb("epsT", [128, 1], F32)
        onesLR = sb("onesLR", [128, 5, 128], BF16)
        wsr = [sb(f"wsr{i}", [128, 1024], BF16) for i in range(2)]
        rstd = [sb(f"rstd{i}", [128, 512], F32) for i in range(2)]
        NTMP = 4
        tmpf = [sb(f"tmpf{i}", [128, 512], F32) for i in range(NTMP)]
        NPT = 5
        ptb = [sb(f"pt{i}", [128, 512], BF16) for i in range(NPT)]
        NBH = 3
        bht = [sb(f"bh{i}", [128, 640], F32) for i in range(NBH)]
        ssv = sb("ssv", [128, 8], F32)
        rv = sb("rv", [128, 8], F32)
        psum = [es.enter_context(nc.psum_tensor(f"ps{i}", [128, 512], F32)) for i in range(8)]

        actT = R1[:, 0:FJ * 1024].rearrange("p (j t) -> p j t", j=FJ)
        v_sb = R1[:, 0:8 * 2048].rearrange("p (w c) -> p w c", w=8)
        KTw = R1[:, 0:8192].rearrange("p (c t) -> p c t", c=8)
        Vw = R1[:, 8192:8192 + 12288].rearrange("p (b c) -> p b c", b=8)
        junk = R1[:, 16384:16384 + 2048]
        sT = R2[:, 0:16384].rearrange("p (c t) -> p c t", c=16)
        QTa = R2[:, 0:8192].rearrange("p (c t) -> p c t", c=8)
        QTb = R2[:, 8192:16384].rearrange("p (c t) -> p c t", c=8)
        OT = R2[:, 16384:24576].rearrange("p (c t) -> p c t", c=8)
        kt_sb = R2[:, 0:8192].rearrange("p (c t) -> p c t", c=8)
        vst = R2[:, 8192:8192 + 12288].rearrange("p (b c) -> p b c", b=8)
        vst5 = R2[:, 8192:8192 + 12288].rearrange("p (b q s d) -> p b q s d", b=8, q=8, s=3)

        reg1, reg2 = Region(), Region()

        xb = [[Buf() for _ in range(2)] for _ in range(KC)]
        hb = [[Buf() for _ in range(2)] for _ in range(KC)]
        ringb = [Buf() for _ in range(RING)]
        psb = [Buf() for _ in range(8)]
        b_const = Buf()
        b_wsT = Buf()
        wsrb = [Buf() for _ in range(2)]
        rstdb = [Buf() for _ in range(2)]
        tmpb = [Buf() for _ in range(NTMP)]
        ptbb = [Buf() for _ in range(NPT)]
        bhb = [Buf() for _ in range(NBH)]
        ssvb = Buf()
        rvb = Buf()
        kdb = [Buf() for _ in range(NTOK // 512)]
        vdb = [Buf() for _ in range(NTOK // 512)]
        outb = Buf()

        rr = {"psA": 0, "psB": 0, "tmp": 0, "pt": 0, "bh": 0, "sq": 0, "wsr": 0, "osb": 0}

        def ps_get(pool):
            if pool == "A":
                i = rr["psA"] % 4
                rr["psA"] += 1
            else:
                i = 4 + rr["psB"] % 4
                rr["psB"] += 1
            return psum[i], psb[i]

        def rot(key, n):
            i = rr[key] % n
            rr[key] += 1
            return i

        P.op(DVE, lambda e: e.memset(ones_bf[:], 1.0 / D), writes=[b_const])
        P.op(DVE, lambda e: e.memset(epsT[:], EPS), writes=[b_const])
        P.op(DVE, lambda e: e.memset(onesLR[:, :, :], 0.0), writes=[b_const])
        P.op(DVE, lambda e: e.memset(onesLR[:, 0, 0:64], 1.0), writes=[b_const])
        P.op(DVE, lambda e: e.memset(onesLR[:, 1, 64:128], 1.0), writes=[b_const])
        P.dma(SP, gains[:], gains_d, writes=[b_const])
        P.dma(SP, bsr[:], bsr_d, writes=[b_const])
        P.dma(SP, flag[:], flag_d, writes=[b_const])
        P.op(DVE, lambda e: e.tensor_scalar(out=onesLR[:, 2:4, :], in0=onesLR[:, 0:2, :], scalar1=flag[:, 0:1], scalar2=None,
                                             op0=ALU.mult), reads=[b_const], writes=[b_const])
        P.dma(POOL, wsT[:, :, :], wsT_d.rearrange("p l g i -> p l (g i)"), writes=[b_wsT])
        wsT4 = wsT[:].rearrange("p l (g i) -> p l g i", g=8)
        P.op(DVE, lambda e: e.memset(wsT4[64:128, :, :, 0:64], 0.0), writes=[b_wsT])

        tiles = [(0, 512, True), (512, 1024, False), (1536, 1024, False)]
        phase_slots = [SL_A, SL_A, SL_KV, SL_B, SL_B]

        def slots_for(tile_is_halo, nph):
            seq = []
            base = 0
            for li in range(2):
                if nph > 2 * li:
                    seq += list(range(base, base + 12))
                if nph > 2 * li + 1:
                    seq += list(range(base + 12, base + SL_A))
                base += SL_A
            if nph > 4:
                seq += list(range(base, base + SL_KV))
            base += SL_KV
            if not tile_is_halo:
                for li in range(2):
                    if nph > 5 + 2 * li:
                        seq += list(range(base, base + 4))
                    if nph > 6 + 2 * li:
                        seq += list(range(base + 4, base + SL_B))
                    base += SL_B
            return seq

        slot_cols = []
        for li_ in range(2):
            slot_cols += [4096] * 23 + [2816] * 8
        slot_cols += [4096] * 4
        for li_ in range(2):
            slot_cols += [4096] * 15 + [2816] * 8
        assert len(slot_cols) == NSLOT
        wseq = []
        for (_, _, ih) in tiles:
            wseq += slots_for(ih, nphases)
        wstate = {"issued": 0, "used": 0}

        def w_issue():
            n = wstate["issued"]
            if n >= len(wseq):
                return
            sid = wseq[n]
            r = n % RING
            ncol = slot_cols[sid]
            P.dma(POOL, ring[r][:, 0:ncol], ws_d[sid][:, 0:ncol], writes=[ringb[r]])
            wstate["issued"] = n + 1

        def w_next(expect_sid):
            n = wstate["used"]
            assert wseq[n] == expect_sid, (n, wseq[n], expect_sid)
            wstate["used"] = n + 1
            r = n % RING
            return ring[r], ringb[r]

        def w_done():
            w_issue()

        for _ in range(RING):
            w_issue()

        def hs(h):
            return slice(h * 512, (h + 1) * 512)

        def norm(nh, gcol, final=False):
            for h in range(nh):
                si = rot("sq", 2)
                for k in range(KC):
                    P.op(ACT, lambda e, k=k: e.activation(out=hT[:, k, hs(h)], in_=xT[:, k, hs(h)], func=AF.Square),
                         reads=[xb[k][h]], writes=[hb[k][h]])
                ps, pb = ps_get("A")
                for k in range(KC):
                    fn = [lambda e, k=k: e.matmul(ps[:, :], ones_bf[:, :], hT[:, k, hs(h)], start=(k == 0), stop=(k == KC - 1))]
                    if k == 0:
                        P.mm(fn, reads=[hb[k][h], b_const], writes=[pb])
                    else:
                        P.mm(fn, reads=[hb[k][h], b_const], acc=[pb])
                ri = si
                P.op(ACT, lambda e: e.activation(out=rstd[ri][:, :], in_=ps[:, :], func=AF.Ln, bias=epsT[:, 0:1], scale=1.0),
                     reads=[pb, b_const], writes=[rstdb[ri]])
                P.op(ACT, lambda e: e.activation(out=rstd[ri][:, :], in_=rstd[ri][:, :], func=AF.Exp, scale=-0.5),
                     reads=[rstdb[ri]], writes=[rstdb[ri]])
                for k in range(KC):
                    if final:
                        P.op(DVE, lambda e, k=k: e.scalar_tensor_tensor(
                            out=xT[:, k, hs(h)], in0=xT[:, k, hs(h)], scalar=gains[:, gcol + k: gcol + k + 1],
                            in1=rstd[ri][:, :], op0=ALU.mult, op1=ALU.mult),
                            reads=[rstdb[ri], b_const, xb[k][h]], writes=[xb[k][h]])
                    else:
                        P.op(DVE, lambda e, k=k: e.scalar_tensor_tensor(
                            out=hT[:, k, hs(h)], in0=xT[:, k, hs(h)], scalar=gains[:, gcol + k: gcol + k + 1],
                            in1=rstd[ri][:, :], op0=ALU.mult, op1=ALU.mult),
                            reads=[rstdb[ri], b_const, xb[k][h]], writes=[hb[k][h]])

        def resid_add(ps, pb, d, h):
            P.op(DVE, lambda e: e.tensor_tensor(out=xT[:, d, hs(h)], in0=ps[:, :], in1=xT[:, d, hs(h)], op=ALU.add),
                 reads=[pb, xb[d][h]], writes=[xb[d][h]])

        def proj_fm(wt, wb, col0, kstride, nk, rhs_fn, rhs_bufs, h):
            ps, pb = ps_get("A")
            P.mm([(lambda e, k=k: e.matmul(ps[:, :], wt[:, k * kstride + col0: k * kstride + col0 + 128], rhs_fn(k),
                                           start=(k == 0), stop=(k == nk - 1))) for k in range(nk)],
                 reads=[wb] + rhs_bufs, writes=[pb])
            return ps, pb

        def phase_A(li, nh, base):
            nw = nh * 4
            norm(nh, G_A + 8 * li)
            vb = reg1.new_bufs(nw * 4)
            junkb = reg1.more_bufs(1)[0]
            sbuf_ = reg2.new_bufs(16 * nw)

            def vbuf(w, s):
                return vb[w * 4 + s]

            def sbf(c, w):
                return sbuf_[c * nw + w]

            def v_item(s, w, wt, wb):
                ps, pb = ps_get("A")
                P.mm([(lambda e, k=k: e.matmul(ps[:, :], hT[:, k, w * 128:(w + 1) * 128], wt[:, k * 512:(k + 1) * 512],
                                               start=(k == 0), stop=(k == KC - 1))) for k in range(KC)],
                     reads=[wb] + [hb[k][w // 4] for k in range(KC)], writes=[pb])
                P.op(ACT, lambda e: e.activation(out=v_sb[:, w, s * 512:(s + 1) * 512], in_=ps[:, :], func=AF.Gelu),
                     reads=[pb], writes=[vbuf(w, s)])
                if s == 3:
                    P.op(ACT, lambda e: e.activation(out=junk[:, :], in_=v_sb[:, w, :], func=AF.Square,
                                                     accum_out=ssv[:, w:w + 1]),
                         reads=[vbuf(w, s_) for s_ in range(4)], writes=[junkb, ssvb])

            def rv_half(h):
                P.op(ACT, lambda e: e.activation(out=rv[:, 4 * h:4 * h + 4], in_=ssv[:, 4 * h:4 * h + 4], func=AF.Ln,
                                                 bias=epsT[:, 0:1], scale=1.0 / ADFF),
                     reads=[ssvb, b_const], writes=[rvb])
                P.op(ACT, lambda e: e.activation(out=rv[:, 4 * h:4 * h + 4], in_=rv[:, 4 * h:4 * h + 4], func=AF.Exp, scale=-0.5),
                     reads=[rvb], writes=[rvb])

            def spatial(w):
                wi = rot("wsr", 2)
                P.op(DVE, lambda e: e.tensor_scalar(out=wsr[wi][:, :], in0=wsT[:, li, :], scalar1=rv[:, w:w + 1],
                                                     scalar2=None, op0=ALU.mult),
                     reads=[b_wsT, rvb], writes=[wsrb[wi]])
                for cq in range(4):
                    ps, pb = ps_get("B")
                    P.mm([(lambda e, cc=cc: e.matmul(ps[:, cc * 128:(cc + 1) * 128],
                                                     v_sb[:, w, (cq * 4 + cc) * 128:(cq * 4 + cc + 1) * 128],
                                                     wsr[wi][:, ((cq * 4 + cc) // 2) * 128:((cq * 4 + cc) // 2 + 1) * 128],
                                                     start=True, stop=True)) for cc in range(4)],
                         reads=[wsrb[wi]] + [vbuf(w, s_) for s_ in range(4)], writes=[pb])
                    for cc in range(4):
                        c = cq * 4 + cc
                        P.op(DVE, lambda e, c=c, cc=cc: e.scalar_tensor_tensor(
                            out=sT[:, c, w * 128:(w + 1) * 128], in0=ps[:, cc * 128:(cc + 1) * 128],
                            scalar=gains[:, G_SGU + 16 * li + c: G_SGU + 16 * li + c + 1],
                            in1=bsr[:, li, c // 2, :], op0=ALU.mult, op1=ALU.add),
                            reads=[pb, b_const], writes=[sbf(c, w)])

            def u_item_pe(s, h, cc, wt, wb):
                c = s * 4 + cc
                ps, pb = proj_fm(wt, wb, cc * 128, 512, KC, lambda k: hT[:, k, hs(h)], [hb[k][h] for k in range(KC)], h)
                ti = rot("tmp", NTMP)
                P.op(ACT, lambda e: e.activation(out=tmpf[ti][:, :], in_=ps[:, :], func=AF.Gelu), reads=[pb], writes=[tmpb[ti]])
                return (c, h, ti)

            def u_item_mul(item):
                c, h, ti = item
                wl = [sbf(c, w) for w in range(h * 4, h * 4 + 4)]
                P.op(DVE, lambda e: e.tensor_tensor(out=sT[:, c, hs(h)], in0=tmpf[ti][:, :], in1=sT[:, c, hs(h)], op=ALU.mult),
                     reads=[tmpb[ti]] + wl, writes=wl)

            def u_item(s, h, cc, wt, wb):
                u_item_mul(u_item_pe(s, h, cc, wt, wb))

            for s in range(3):
                wt, wb = w_next(base + s)
                for w in range(nw):
                    v_item(s, w, wt, wb)
                w_done()
            wt, wb = w_next(base + 3)
            for w in range(4):
                v_item(3, w, wt, wb)
            rv_half(0)
            if nh == 1:
                w_done()
                uw = w_next(base + 4)
                pend = []
                for w in range(4):
                    spatial(w)
                    pend.append(u_item_pe(0, 0, w, *uw))
                for it in pend:
                    u_item_mul(it)
                w_done()
            else:
                for w in range(4):
                    spatial(w)
                    v_item(3, 4 + w, wt, wb)
                w_done()
                rv_half(1)
                uw = w_next(base + 4)
                for w in range(4):
                    spatial(4 + w)
                    u_item(0, 0, w, *uw)
                for cc in range(4):
                    u_item(0, 1, cc, *uw)
                w_done()
            for s in range(1, 4):
                wt, wb = w_next(base + 4 + s)
                for h, cc in [(h_, c_) for h_ in range(nh) for c_ in range(4)]:
                    u_item(s, h, cc, wt, wb)
                w_done()
            for s in range(4):
                wt, wb = w_next(base + 8 + s)
                for dd in range(2):
                    d = 2 * s + dd
                    for h in range(nh):
                        rb = []
                        for c in range(16):
                            rb += [sbf(c, w) for w in range(h * 4, h * 4 + 4)]
                        ps, pb = proj_fm(wt, wb, dd * 128, 256, 16, lambda j: sT[:, j, hs(h)], rb, h)
                        resid_add(ps, pb, d, h)
                w_done()

        def phase_F(layer, nh, base):
            norm(nh, G_F + 8 * layer)
            ab = reg1.new_bufs(FJ * nh)

            def abf(j, h):
                return ab[j * nh + h]

            for s in range(11):
                wt, wb = w_next(base + s)
                for h, jj in [(h_, j_) for h_ in range(nh) for j_ in range(2)]:
                    j = 2 * s + jj
                    if True:
                        if 'nogu' in KDBG:
                            continue
                        rbufs = [hb[k][h] for k in range(KC)]
                        psg, pbg = proj_fm(wt, wb, jj * 128, 512, KC, lambda k: hT[:, k, hs(h)], rbufs, h)
                        psu, pbu = proj_fm(wt, wb, 256 + jj * 128, 512, KC, lambda k: hT[:, k, hs(h)], rbufs, h)
                        ti = rot("tmp", NTMP)
                        P.op(ACT, lambda e: e.activation(out=tmpf[ti][:, :], in_=psg[:, :], func=AF.Silu),
                             reads=[pbg], writes=[tmpb[ti]])
                        P.op(DVE, lambda e: e.tensor_tensor(out=actT[:, j, hs(h)], in0=psu[:, :], in1=tmpf[ti][:, :],
                                                             op=ALU.mult), reads=[pbu, tmpb[ti]], writes=[abf(j, h)])
                w_done()
            for d in range(8):
                wt, wb = w_next(base + 11 + d)
                for h in range(nh):
                    if 'nodown' in KDBG:
                        continue
                    ps, pb = proj_fm(wt, wb, 0, 128, FJ, lambda j: actT[:, j, hs(h)], [abf(j, h) for j in range(FJ)], h)
                    resid_add(ps, pb, d, h)
                w_done()

        def phase_KV(tok0, nh, is_halo, base, after_norm=None):
            nb = nh * 4
            norm(nh, G_KV)
            if after_norm is not None:
                after_norm()
            kb_ = reg2.new_bufs(KC * nh)
            vsb_ = reg2.more_bufs(nb * 2 + 1)
            onesb = vsb_[-1]
            P.op(DVE, lambda e: e.memset(vst5[:, 0:nb, :, 1, :], 0.0), writes=[onesb])
            for s in range(2):
                wt, wb = w_next(base + s)
                for h, cc in [(h_, c_) for h_ in range(nh) for c_ in range(4)]:
                    c = 4 * s + cc
                    if True:
                        ps, pb = proj_fm(wt, wb, cc * 128, 512, KC, lambda k: hT[:, k, hs(h)],
                                         [hb[k][h] for k in range(KC)], h)
                        P.op(ACT, lambda e: e.activation(out=kt_sb[:, c, hs(h)], in_=ps[:, :], func=AF.Copy),
                             reads=[pb], writes=[kb_[c * nh + h]])
                w_done()
            for s in range(2):
                wt, wb = w_next(base + 2 + s)
                for blk in range(nb):
                    ps, pb = ps_get("A")
                    P.mm([(lambda e, k=k: e.matmul(ps[:, :], hT[:, k, blk * 128:(blk + 1) * 128], wt[:, k * 512:(k + 1) * 512],
                                                   start=(k == 0), stop=(k == KC - 1))) for k in range(KC)],
                         reads=[wb] + [hb[k][blk // 4] for k in range(KC)], writes=[pb])
                    ps4 = ps[:, :].rearrange("p (q s d) -> p q s d", q=4, s=2)
                    P.op(DVE, lambda e: e.tensor_copy(out=vst5[:, blk, 4 * s:4 * s + 4, 0, :], in_=ps4[:, :, 0, :]),
                         reads=[pb], writes=[vsb_[blk * 2 + s]])
                    P.op(ACT, lambda e: e.activation(out=vst5[:, blk, 4 * s:4 * s + 4, 2, :], in_=ps4[:, :, 1, :],
                                                     func=AF.Copy), reads=[pb], writes=[vsb_[blk * 2 + s]])
                w_done()
            for h in range(nh):
                bi = tok0 // 512 + h
                t0 = tok0 + h * 512
                P.dma(SP, kT_d[:, :, t0:t0 + 512], kt_sb[:, :, hs(h)],
                      reads=[kb_[c * nh + h] for c in range(KC)], writes=[kdb[bi]])
                P.dma(SP, v_d[:, t0 // 128: t0 // 128 + 4, :], vst[:, h * 4:(h + 1) * 4, :],
                      reads=[vsb_[blk * 2 + s] for blk in range(h * 4, h * 4 + 4) for s in range(2)] + [onesb],
                      writes=[vdb[bi]])

        def phase_B(li, tok0, nh, base):
            norm(nh, G_B + 8 * li)
            qb = reg2.new_bufs(KC * nh)
            qb2 = reg2.more_bufs(KC * nh)
            ob = reg2.more_bufs(KC * nh)
            for s in range(2):
                wt, wb = w_next(base + s)
                for h, cc in [(h_, c_) for h_ in range(nh) for c_ in range(4)]:
                    c = 4 * s + cc
                    if True:
                        ps, pb = proj_fm(wt, wb, cc * 128, 512, KC, lambda k: hT[:, k, hs(h)],
                                         [hb[k][h] for k in range(KC)], h)
                        P.op(ACT, lambda e: e.activation(out=QTa[:, c, hs(h)], in_=ps[:, :], func=AF.Copy, scale=0.125),
                             reads=[pb], writes=[qb[c * nh + h]])
                        P.op(ACT, lambda e: e.activation(out=QTb[:, c, hs(h)], in_=ps[:, :], func=AF.Copy, scale=0.125),
                             reads=[pb], writes=[qb2[c * nh + h]])
                        P.op(DVE, lambda e: e.memset(QTa[64:128, c, hs(h)], 0.0), writes=[qb[c * nh + h]])
                        P.op(DVE, lambda e: e.memset(QTb[0:64, c, hs(h)], 0.0), writes=[qb2[c * nh + h]])
                w_done()
            kws = reg1.new_bufs(2)
            vws = reg1.more_bufs(2)
            for h in range(nh):
                bi = tok0 // 512 + h
                t0 = tok0 + h * 512
                sp_, so_ = h % 2, 1 - h % 2
                if h == 0:
                    P.dma(SP, KTw[:, :, sp_ * 512:(sp_ + 1) * 512], kT_d[:, :, t0 - 512:t0], reads=[kdb[bi - 1]], writes=[kws[sp_]])
                    P.dma(SP, Vw[:, sp_ * 4:(sp_ + 1) * 4, :], v_d[:, (t0 - 512) // 128:t0 // 128, :], reads=[vdb[bi - 1]],
                          writes=[vws[sp_]])
                P.dma(SP, KTw[:, :, so_ * 512:(so_ + 1) * 512], kT_d[:, :, t0:t0 + 512], reads=[kdb[bi]], writes=[kws[so_]])
                P.dma(SP, Vw[:, so_ * 4:(so_ + 1) * 4, :], v_d[:, t0 // 128:(t0 + 512) // 128, :], reads=[vdb[bi]],
                      writes=[vws[so_]])

                def kslot(kb):
                    return (sp_ if kb < 4 else so_)

                def kblk(kb):
                    return kslot(kb) * 4 + kb % 4
                steps = [(hd, kb) for hd in range(NHEAD) for kb in range(8)]
                st = {}
                LA = 3

                def emit_S(i):
                    hd, kb = steps[i]
                    c, po = hd // 2, (hd % 2) * 64
                    if kb == 0:
                        bi_ = rot("bh", NBH)
                        P.dma(SP, bht[bi_][:, :], bias_d[li, hd], writes=[bhb[bi_]])
                        st[("bh", hd)] = bi_
                    bi_ = st[("bh", hd)]
                    i_lo, i_hi = max(0, 2 * kb - 8), min(7, 2 * kb + 1)
                    nq = i_hi - i_lo + 1
                    m_lo = 8 + i_lo - 2 * kb
                    ps, pb = ps_get("A")
                    qt_, qtb_ = (QTa, qb) if hd % 2 == 0 else (QTb, qb2)
                    P.mm([lambda e: e.matmul(ps[:, 0:nq * 64], KTw[:, c, kblk(kb) * 128:(kblk(kb) + 1) * 128],
                                             qt_[:, c, h * 512 + i_lo * 64: h * 512 + (i_hi + 1) * 64],
                                             start=True, stop=True)],
                         reads=[kws[kslot(kb)], qtb_[c * nh + h]], writes=[pb])
                    ti = rot("tmp", NTMP)
                    P.op(DVE, lambda e: e.tensor_tensor(out=tmpf[ti][:, 0:nq * 64], in0=ps[:, 0:nq * 64],
                                                         in1=bht[bi_][:, m_lo * 64:(m_lo + nq) * 64], op=ALU.add),
                         reads=[pb, bhb[bi_]], writes=[tmpb[ti]])
                    pi = rot("pt", NPT)
                    P.op(ACT, lambda e: e.activation(out=ptb[pi][:, 0:nq * 64], in_=tmpf[ti][:, 0:nq * 64], func=AF.Exp),
                         reads=[tmpb[ti]], writes=[ptbb[pi]])
                    st[("pt", i)] = (pi, i_lo)

                def emit_PV(i):
                    hd, kb = steps[i]
                    c, po = hd // 2, (hd % 2) * 64
                    pi, i_lo = st.pop(("pt", i))
                    i_hi = min(7, 2 * kb + 1)
                    if kb == 0 and hd % 2 == 0:
                        st[("o", c)] = (ps_get("B"), ps_get("B"))
                    (nps, npb), (dps, dpb) = st[("o", c)]
                    pt = ptb[pi]
                    vc0 = c * 192 + (hd % 2) * 64
                    halo_keys = (bi - 1 == 0) and kb < 4
                    oc = (2 if halo_keys else 0) + (hd % 2)
                    first = (kb == 0 and hd % 2 == 0)
                    last = (kb == 7 and hd % 2 == 1)

                    def col(iq):
                        return (iq - i_lo) * 64

                    segs = [(i_lo, i_hi + 1, 0, 128)]
                    fns = []
                    if first:
                        fns.append(lambda e: e.matmul(nps[:, :], onesLR[:, 4, :], QTa[:, c, hs(h)], start=True, stop=False))
                        fns.append(lambda e: e.matmul(dps[:, :], onesLR[:, 4, :], QTa[:, c, hs(h)], start=True, stop=False))
                    for si, (c0, c1, p0, p1) in enumerate(segs):
                        fns.append(lambda e, c0=c0, c1=c1, p0=p0, p1=p1, si=si: e.matmul(
                            nps[:, c0 * 64:c1 * 64], Vw[p0:p1, kblk(kb), vc0:vc0 + 128], pt[p0:p1, col(c0):col(c1)],
                            start=False, stop=(last and si == len(segs) - 1)))
                    for si, (c0, c1, p0, p1) in enumerate(segs):
                        fns.append(lambda e, c0=c0, c1=c1, p0=p0, p1=p1, si=si: e.matmul(
                            dps[:, c0 * 64:c1 * 64], onesLR[p0:p1, oc, :], pt[p0:p1, col(c0):col(c1)],
                            start=False, stop=(last and si == len(segs) - 1)))
                    if first:
                        P.mm(fns, reads=[ptbb[pi], vws[kslot(kb)], b_const, qb[c * nh + h]], writes=[npb, dpb])
                    else:
                        P.mm(fns, reads=[ptbb[pi], vws[kslot(kb)], b_const], acc=[npb, dpb])
                    if last and 'nonorm' not in KDBG:
                        ti = rot("tmp", NTMP)
                        P.op(ACT, lambda e: e.activation(out=tmpf[ti][:, :], in_=dps[:, :], func=AF.Ln), reads=[dpb], writes=[tmpb[ti]])
                        P.op(ACT, lambda e: e.activation(out=tmpf[ti][:, :], in_=tmpf[ti][:, :], func=AF.Exp, scale=-1.0),
                             reads=[tmpb[ti]], writes=[tmpb[ti]])
                        P.op(DVE, lambda e: e.tensor_tensor(out=OT[:, c, hs(h)], in0=nps[:, :], in1=tmpf[ti][:, :], op=ALU.mult),
                             reads=[npb, tmpb[ti]], writes=[ob[c * nh + h]])
                        del st[("o", c)]

                n = len(steps)
                if 'noattn' in KDBG:
                    n = -LA
                for i in range(n + LA):
                    if i < n:
                        emit_S(i)
                    if i - LA >= 0 and 'nopv' not in KDBG:
                        emit_PV(i - LA)
            for s in range(2):
                wt, wb = w_next(base + 2 + s)
                for dd in range(4):
                    d = 4 * s + dd
                    for h in range(nh):
                        ps, pb = proj_fm(wt, wb, dd * 128, 512, KC, lambda c: OT[:, c, hs(h)],
                                         [ob[c * nh + h] for c in range(KC)], h)
                        resid_add(ps, pb, d, h)
                w_done()

        prefetched = False
        loaded_early = set()
        for (tok0, T, is_halo) in tiles:
            nh = T // 512
            if not ((prefetched and tok0 == HALO) or tok0 in loaded_early):
                for h in range(nh):
                    P.dma(SP, xT[:, :, hs(h)], x_d[:, :, tok0 + h * 512: tok0 + (h + 1) * 512],
                          writes=[xb[k][h] for k in range(KC)])
            ph = 0
            base = 0
            for li in range(2):
                if nphases > ph:
                    phase_A(li, nh, base)
                ph += 1
                if nphases > ph:
                    phase_F(li, nh, base + 12)
                ph += 1
                base += SL_A
            if nphases > ph:
                def prefetch_next_x():
                    for h_ in range(2):
                        P.dma(ACT, xT[:, :, hs(h_)], x_d[:, :, HALO + h_ * 512: HALO + (h_ + 1) * 512],
                              writes=[xb[k][h_] for k in range(KC)])
                if is_halo:
                    prefetched = True
                phase_KV(tok0, nh, is_halo, base, after_norm=prefetch_next_x if is_halo else None)
            ph += 1
            base += SL_KV
            if is_halo:
                continue
            for li in range(2):
                if nphases > ph:
                    phase_B(li, tok0, nh, base)
                ph += 1
                if nphases > ph:
                    phase_F(2 + li, nh, base + 4)
                ph += 1
                base += SL_B
            if nphases > ph:
                norm(nh, G_FIN, final=True)
            nxt = tok0 + T
            for h in range(nh):
                o0 = tok0 - HALO + h * 512
                P.dma(SP, out_d[:, :, o0:o0 + 512], xT[:, :, hs(h)], reads=[xb[k][h] for k in range(KC)])
                if nxt < NTOK:
                    P.dma(SP, xT[:, :, hs(h)], x_d[:, :, nxt + h * 512: nxt + (h + 1) * 512],
                          writes=[xb[k][h] for k in range(KC)])
                    loaded_early.add(nxt)
        assert wstate["used"] == len(wseq), (wstate, len(wseq))
        for sem, val in SP.dsem:
            if val > 0:
                nc.sync.wait_ge(sem, val)
    return nc


_CACHE = {}


def _prep(inputs):
    inp = {k: np.asarray(v, dtype=np.float32) for k, v in inputs.items()}
    wstream = _build_wstream(inp)
    gains, bsr, wsT, bias = _build_small(inp)
    x = inp["x"]
    in_maps = []
    for c in range(NCORE):
        b, half = c // 2, c % 2
        xs = np.zeros((NTOK, D), np.float32)
        if half == 1:
            xs[:] = x[b, OWN - HALO: SEQ]
        else:
            xs[HALO:] = x[b, 0:OWN]
        x_t = np.ascontiguousarray(xs.reshape(NTOK, KC, 128).transpose(2, 1, 0))
        flag = np.full((128, 1), float(half), np.float32)
        in_maps.append({"x_t": x_t, "wstream": wstream, "gains": gains, "bsr": bsr, "wsT": wsT,
                        "flag": flag, "relb": bias})
    return in_maps


def _run(inputs, nphases=10, trace=False):
    in_maps = _prep(inputs)
    nc = build(nphases)
    kw = {"trace": True} if trace else {}
    res = run_bass_kernel_spmd(nc, in_maps, core_ids=list(range(NCORE)), **kw)
    out = np.zeros((BATCH, SEQ, D), np.float32)
    for c in range(NCORE):
        b, half = c // 2, c % 2
        o = np.asarray(res.results[c]["out_t"])
        out[b, half * OWN:(half + 1) * OWN] = o.transpose(2, 1, 0).reshape(OWN, D)
    return out, res


def kernel(**inputs):
    out, _ = _run(inputs)
    return out
```

```python
import os
import numpy as np
from contextlib import ExitStack
import concourse.bass as bass
import concourse.mybir as mybir
from concourse.bass_utils import run_bass_kernel_spmd

F32 = mybir.dt.float32
BF16 = mybir.dt.bfloat16
AF = mybir.ActivationFunctionType
ALU = mybir.AluOpType

D = 1024
KC = 8
SEQ = 4096
BATCH = 4
NCORE = 8
OWN = 2048
HALO = 512
NTOK = OWN + HALO
ADFF = 2048
FH = 2816
FJ = 22
EPS = 1e-6
NHEAD = 16
RING = 3
KDBG = os.environ.get('KDBG', '')

G_A = 0
G_F = 16
G_KV = 48
G_B = 56
G_FIN = 72
G_SGU = 80
G_N = 112

SL_A = 31
SL_KV = 4
SL_B = 23
NSLOT = 2 * SL_A + SL_KV + 2 * SL_B


def _kmaj(w):
    k, n = w.shape
    return np.ascontiguousarray(w.reshape(k // 128, 128, n).transpose(1, 0, 2))


def _slot(a):
    a = a.reshape(128, -1)
    out = np.zeros((128, 4096), np.float32)
    out[:, : a.shape[1]] = a
    return out


def _build_wstream(inp):
    slots = []

    def ffn(l):
        wgu = inp["ffn_w_gate_up"][l]
        wd = _kmaj(inp["ffn_w_down"][l])
        for s in range(11):
            cols = np.concatenate([wgu[:, 2 * s * 128:(2 * s + 2) * 128],
                                   wgu[:, FH + 2 * s * 128: FH + (2 * s + 2) * 128]], axis=1)
            slots.append(_slot(_kmaj(cols)))
        for d in range(8):
            slots.append(_slot(wd[:, :, d * 128:(d + 1) * 128]))

    for i in range(2):
        win = inp["a_w_in"][i]
        for s in range(4):
            slots.append(_slot(_kmaj(win[:, ADFF + s * 512: ADFF + (s + 1) * 512])))
        for s in range(4):
            slots.append(_slot(_kmaj(win[:, s * 512:(s + 1) * 512])))
        wo = _kmaj(inp["a_w_out"][i])
        for s in range(4):
            slots.append(_slot(wo[:, :, s * 256:(s + 1) * 256]))
        ffn(i)
    wkv = inp["w_kv"]
    for s in range(2):
        slots.append(_slot(_kmaj(wkv[:, s * 512:(s + 1) * 512])))
    for s in range(2):
        slots.append(_slot(_kmaj(wkv[:, D + s * 512: D + (s + 1) * 512])))
    for i in range(2):
        wq = inp["b_w_q"][i]
        for s in range(2):
            slots.append(_slot(_kmaj(wq[:, s * 512:(s + 1) * 512])))
        wo = inp["b_w_o"][i]
        for s in range(2):
            slots.append(_slot(_kmaj(wo[:, s * 512:(s + 1) * 512])))
        ffn(2 + i)
    assert len(slots) == NSLOT
    return np.stack(slots, 0)


def _gcols(g):
    return np.ascontiguousarray(g.reshape(-1, 128).T)


def _build_small(inp):
    gains = np.zeros((128, G_N), np.float32)
    for i in range(2):
        gains[:, G_A + 8 * i: G_A + 8 * i + 8] = _gcols(inp["a_norm"][i])
        gains[:, G_B + 8 * i: G_B + 8 * i + 8] = _gcols(inp["b_norm"][i])
        gains[:, G_SGU + 16 * i: G_SGU + 16 * i + 16] = _gcols(inp["a_sgu_norm"][i])
    for i in range(4):
        gains[:, G_F + 8 * i: G_F + 8 * i + 8] = _gcols(inp["ffn_norm"][i])
    gains[:, G_KV: G_KV + 8] = _gcols(inp["kv_norm"])
    gains[:, G_FIN: G_FIN + 8] = _gcols(inp["final_norm"])
    bsr = np.ascontiguousarray(np.broadcast_to(inp["a_b_spatial"][None], (128, 2, 8, 128))).astype(np.float32)
    wsT = np.ascontiguousarray(np.transpose(inp["a_w_spatial"], (3, 0, 1, 2))).astype(np.float32)
    kl = np.arange(128)[:, None]
    qq = np.arange(640)[None, :]
    idx = np.clip(qq - kl, -256, 256) + 256
    bias = np.ascontiguousarray(inp["b_rel_bias"][:, :, idx]).astype(np.float32)
    m = (qq // 64)
    kh = (kl // 64)
    valid = (m >= kh) & (m <= 8 + kh)
    bias = np.where(valid[None, None], bias, np.float32(-30000.0)).astype(np.float32)
    return gains, bsr, wsT, bias


class Buf:
    __slots__ = ("w", "r")

    def __init__(self):
        self.w = None
        self.r = {}


class Eng:
    def __init__(self, nc, es, eng, name, ndma=0):
        self.nc, self.es, self.eng, self.name = nc, es, eng, name
        self.gen = 0
        self.sem = es.enter_context(nc.semaphore(f"s_{name}_0"))
        self.cnt = 0
        self.seen = {}
        self.dsem = [[es.enter_context(nc.semaphore(f"d_{name}_{i}")), 0] for i in range(ndma)]
        self.drr = 0

    def next_event(self):
        if self.cnt >= 24000:
            self.gen += 1
            self.sem = self.es.enter_context(self.nc.semaphore(f"s_{self.name}_{self.gen}"))
            self.cnt = 0
        self.cnt += 1
        return (self.sem, self.cnt)


class Prog:
    def __init__(self, nc, es):
        self.nc = nc
        self.pe = Eng(nc, es, nc.tensor, "pe")
        self.act = Eng(nc, es, nc.scalar, "act", ndma=2)
        self.dve = Eng(nc, es, nc.vector, "dve")
        self.pool = Eng(nc, es, nc.gpsimd, "pool", ndma=RING + 2)
        self.sp = Eng(nc, es, nc.sync, "sp", ndma=8)

    @staticmethod
    def _deps(E, reads, writes):
        need = {}

        def add(ev):
            if ev is None:
                return
            s, v = ev
            if need.get(s, 0) < v:
                need[s] = v

        for b in reads:
            add(b.w)
        for b in writes:
            add(b.w)
            for s, v in b.r.items():
                add((s, v))
        out = []
        for s, v in need.items():
            if E.seen.get(s, 0) >= v:
                continue
            E.seen[s] = v
            out.append((s, v))
        return out

    @staticmethod
    def _mark(ev, reads, writes, acc=()):
        s, v = ev
        for b in writes:
            b.w = ev
            b.r = {}
        for b in acc:
            b.w = ev
            b.r = {}
        for b in reads:
            if b.r.get(s, 0) < v:
                b.r[s] = v

    def op(self, E, fn, reads=(), writes=()):
        deps = self._deps(E, reads, writes)
        for s, v in deps[1:]:
            E.eng.wait_ge(s, v)
        ins = fn(E.eng)
        if deps:
            ins._wait_ge(deps[0][0], deps[0][1])
        ev = E.next_event()
        ins.then_inc(ev[0], 1)
        self._mark(ev, reads, writes)

    def mm(self, fns, reads=(), writes=(), acc=()):
        E = self.pe
        deps = self._deps(E, reads, writes)
        for s, v in deps:
            E.eng.wait_ge(s, v)
        ins = None
        for f in fns:
            ins = f(E.eng)
        ev = E.next_event()
        ins.then_inc(ev[0], 1)
        self._mark(ev, reads, writes, acc)

    def dma(self, Q, out, in_, reads=(), writes=()):
        ds = Q.dsem[Q.drr]
        Q.drr = (Q.drr + 1) % len(Q.dsem)
        sem, val = ds
        if val > 0 and Q.seen.get(sem, 0) < val:
            Q.eng.wait_ge(sem, val)
            Q.seen[sem] = val
        for s, v in self._deps(Q, reads, writes):
            Q.eng.wait_ge(s, v)
        Q.eng.dma_start(out=out, in_=in_).then_inc(sem, 16)
        ds[1] = val + 16
        self._mark((sem, val + 16), reads, writes)


class Region:
    def __init__(self):
        self.pending = {}
        self.cur = []

    def new_bufs(self, n):
        for b in self.cur:
            if b.w is not None:
                s, v = b.w
                if self.pending.get(s, 0) < v:
                    self.pending[s] = v
            for s, v in b.r.items():
                if self.pending.get(s, 0) < v:
                    self.pending[s] = v
        self.cur = []
        out = []
        for _ in range(n):
            b = Buf()
            b.r = dict(self.pending)
            out.append(b)
        self.cur = list(out)
        return out

    def more_bufs(self, n):
        out = []
        for _ in range(n):
            b = Buf()
            b.r = dict(self.pending)
            out.append(b)
        self.cur += out
        return out


def build(nphases=10):
    nc = bass.Bass("TRN2", target_bir_lowering=False)
    x_d = nc.dram_tensor("x_t", [128, KC, NTOK], F32, kind="ExternalInput").ap()
    ws_d = nc.dram_tensor("wstream", [NSLOT, 128, 4096], F32, kind="ExternalInput").ap()
    gains_d = nc.dram_tensor("gains", [128, G_N], F32, kind="ExternalInput").ap()
    bsr_d = nc.dram_tensor("bsr", [128, 2, 8, 128], F32, kind="ExternalInput").ap()
    wsT_d = nc.dram_tensor("wsT", [128, 2, 8, 128], F32, kind="ExternalInput").ap()
    flag_d = nc.dram_tensor("flag", [128, 1], F32, kind="ExternalInput").ap()
    bias_d = nc.dram_tensor("relb", [2, NHEAD, 128, 640], F32, kind="ExternalInput").ap()
    out_d = nc.dram_tensor("out_t", [128, KC, OWN], F32, kind="ExternalOutput").ap()
    kT_d = nc.dram_tensor("kT_scr", [128, KC, NTOK], BF16, kind="Internal").ap()
    v_d = nc.dram_tensor("v_scr", [128, NTOK // 128, 1536], BF16, kind="Internal").ap()

    with ExitStack() as es:
        P = Prog(nc, es)
        PE, ACT, DVE, POOL, SP = P.pe, P.act, P.dve, P.pool, P.sp

        def sb(name, shape, dt):
            return es.enter_context(nc.sbuf_tensor(name, shape, dt))

        xT = sb("xT", [128, KC, 1024], F32)
        hT = sb("hT", [128, KC, 1024], BF16)
        R1 = sb("R1", [128, 22528], BF16)
        R2 = sb("R2", [128, 24576], BF16)
        ring = [sb(f"ring{i}", [128, 4096], BF16) for i in range(RING)]
        ones_bf = sb("ones_bf", [128, 128], BF16)
        gains = sb("gains_sb", [128, G_N], F32)
        bsr = sb("bsr_sb", [128, 2, 8, 128], F32)
        wsT = sb("wsT_sb", [128, 2, 1024], BF16)
        flag = sb("flag_sb", [128, 1], F32)
        epsT = sb("epsT", [128, 1], F32)
        onesLR = sb("onesLR", [128, 5, 128], BF16)
        wsr = [sb(f"wsr{i}", [128, 1024], BF16) for i in range(2)]
        rstd = [sb(f"rstd{i}", [128, 512], F32) for i in range(2)]
        NTMP = 4
        tmpf = [sb(f"tmpf{i}", [128, 512], F32) for i in range(NTMP)]
        NPT = 5
        ptb = [sb(f"pt{i}", [128, 512], BF16) for i in range(NPT)]
        NBH = 3
        bht = [sb(f"bh{i}", [128, 640], F32) for i in range(NBH)]
        ssv = sb("ssv", [128, 8], F32)
        rv = sb("rv", [128, 8], F32)
        psum = [es.enter_context(nc.psum_tensor(f"ps{i}", [128, 512], F32)) for i in range(8)]

        actT = R1[:, 0:FJ * 1024].rearrange("p (j t) -> p j t", j=FJ)
        v_sb = R1[:, 0:8 * 2048].rearrange("p (w c) -> p w c", w=8)
        KTw = R1[:, 0:8192].rearrange("p (c t) -> p c t", c=8)
        Vw = R1[:, 8192:8192 + 12288].rearrange("p (b c) -> p b c", b=8)
        junk = R1[:, 16384:16384 + 2048]
        sT = R2[:, 0:16384].rearrange("p (c t) -> p c t", c=16)
        QTa = R2[:, 0:8192].rearrange("p (c t) -> p c t", c=8)
        QTb = R2[:, 8192:16384].rearrange("p (c t) -> p c t", c=8)
        OT = R2[:, 16384:24576].rearrange("p (c t) -> p c t", c=8)
        kt_sb = R2[:, 0:8192].rearrange("p (c t) -> p c t", c=8)
        vst = R2[:, 8192:8192 + 12288].rearrange("p (b c) -> p b c", b=8)
        vst5 = R2[:, 8192:8192 + 12288].rearrange("p (b q s d) -> p b q s d", b=8, q=8, s=3)

        reg1, reg2 = Region(), Region()

        xb = [[Buf() for _ in range(2)] for _ in range(KC)]
        hb = [[Buf() for _ in range(2)] for _ in range(KC)]
        ringb = [Buf() for _ in range(RING)]
        psb = [Buf() for _ in range(8)]
        b_const = Buf()
        b_wsT = Buf()
        wsrb = [Buf() for _ in range(2)]
        rstdb = [Buf() for _ in range(2)]
        tmpb = [Buf() for _ in range(NTMP)]
        ptbb = [Buf() for _ in range(NPT)]
        bhb = [Buf() for _ in range(NBH)]
        ssvb = Buf()
        rvb = Buf()
        kdb = [Buf() for _ in range(NTOK // 512)]
        vdb = [Buf() for _ in range(NTOK // 512)]
        outb = Buf()

        rr = {"psA": 0, "psB": 0, "tmp": 0, "pt": 0, "bh": 0, "sq": 0, "wsr": 0, "osb": 0}

        def ps_get(pool):
            if pool == "A":
                i = rr["psA"] % 4
                rr["psA"] += 1
            else:
                i = 4 + rr["psB"] % 4
                rr["psB"] += 1
            return psum[i], psb[i]

        def rot(key, n):
            i = rr[key] % n
            rr[key] += 1
            return i

        P.op(DVE, lambda e: e.memset(ones_bf[:], 1.0 / D), writes=[b_const])
        P.op(DVE, lambda e: e.memset(epsT[:], EPS), writes=[b_const])
        P.op(DVE, lambda e: e.memset(onesLR[:, :, :], 0.0), writes=[b_const])
        P.op(DVE, lambda e: e.memset(onesLR[:, 0, 0:64], 1.0), writes=[b_const])
        P.op(DVE, lambda e: e.memset(onesLR[:, 1, 64:128], 1.0), writes=[b_const])
        P.dma(SP, gains[:], gains_d, writes=[b_const])
        P.dma(SP, bsr[:], bsr_d, writes=[b_const])
        P.dma(SP, flag[:], flag_d, writes=[b_const])
        P.op(DVE, lambda e: e.tensor_scalar(out=onesLR[:, 2:4, :], in0=onesLR[:, 0:2, :], scalar1=flag[:, 0:1], scalar2=None,
                                             op0=ALU.mult), reads=[b_const], writes=[b_const])
        P.dma(POOL, wsT[:, :, :], wsT_d.rearrange("p l g i -> p l (g i)"), writes=[b_wsT])
        wsT4 = wsT[:].rearrange("p l (g i) -> p l g i", g=8)
        P.op(DVE, lambda e: e.memset(wsT4[64:128, :, :, 0:64], 0.0), writes=[b_wsT])

        tiles = [(0, 512, True), (512, 1024, False), (1536, 1024, False)]
        phase_slots = [SL_A, SL_A, SL_KV, SL_B, SL_B]

        def slots_for(tile_is_halo, nph):
            seq = []
            base = 0
            for li in range(2):
                if nph > 2 * li:
                    seq += list(range(base, base + 12))
                if nph > 2 * li + 1:
                    seq += list(range(base + 12, base + SL_A))
                base += SL_A
            if nph > 4:
                seq += list(range(base, base + SL_KV))
            base += SL_KV
            if not tile_is_halo:
                for li in range(2):
                    if nph > 5 + 2 * li:
                        seq += list(range(base, base + 4))
                    if nph > 6 + 2 * li:
                        seq += list(range(base + 4, base + SL_B))
                    base += SL_B
            return seq

        slot_cols = []
        for li_ in range(2):
            slot_cols += [4096] * 23 + [2816] * 8
        slot_cols += [4096] * 4
        for li_ in range(2):
            slot_cols += [4096] * 15 + [2816] * 8
        assert len(slot_cols) == NSLOT
        wseq = []
        for (_, _, ih) in tiles:
            wseq += slots_for(ih, nphases)
        wstate = {"issued": 0, "used": 0}

        def w_issue():
            n = wstate["issued"]
            if n >= len(wseq):
                return
            sid = wseq[n]
            r = n % RING
            ncol = slot_cols[sid]
            P.dma(POOL, ring[r][:, 0:ncol], ws_d[sid][:, 0:ncol], writes=[ringb[r]])
            wstate["issued"] = n + 1

        def w_next(expect_sid):
            n = wstate["used"]
            assert wseq[n] == expect_sid, (n, wseq[n], expect_sid)
            wstate["used"] = n + 1
            r = n % RING
            return ring[r], ringb[r]

        def w_done():
            w_issue()

        for _ in range(RING):
            w_issue()

        def hs(h):
            return slice(h * 512, (h + 1) * 512)

        def norm(nh, gcol, final=False):
            for h in range(nh):
                si = rot("sq", 2)
                for k in range(KC):
                    P.op(ACT, lambda e, k=k: e.activation(out=hT[:, k, hs(h)], in_=xT[:, k, hs(h)], func=AF.Square),
                         reads=[xb[k][h]], writes=[hb[k][h]])
                ps, pb = ps_get("A")
                for k in range(KC):
                    fn = [lambda e, k=k: e.matmul(ps[:, :], ones_bf[:, :], hT[:, k, hs(h)], start=(k == 0), stop=(k == KC - 1))]
                    if k == 0:
                        P.mm(fn, reads=[hb[k][h], b_const], writes=[pb])
                    else:
                        P.mm(fn, reads=[hb[k][h], b_const], acc=[pb])
                ri = si
                P.op(ACT, lambda e: e.activation(out=rstd[ri][:, :], in_=ps[:, :], func=AF.Ln, bias=epsT[:, 0:1], scale=1.0),
                     reads=[pb, b_const], writes=[rstdb[ri]])
                P.op(ACT, lambda e: e.activation(out=rstd[ri][:, :], in_=rstd[ri][:, :], func=AF.Exp, scale=-0.5),
                     reads=[rstdb[ri]], writes=[rstdb[ri]])
                for k in range(KC):
                    if final:
                        P.op(DVE, lambda e, k=k: e.scalar_tensor_tensor(
                            out=xT[:, k, hs(h)], in0=xT[:, k, hs(h)], scalar=gains[:, gcol + k: gcol + k + 1],
                            in1=rstd[ri][:, :], op0=ALU.mult, op1=ALU.mult),
                            reads=[rstdb[ri], b_const, xb[k][h]], writes=[xb[k][h]])
                    else:
                        P.op(DVE, lambda e, k=k: e.scalar_tensor_tensor(
                            out=hT[:, k, hs(h)], in0=xT[:, k, hs(h)], scalar=gains[:, gcol + k: gcol + k + 1],
                            in1=rstd[ri][:, :], op0=ALU.mult, op1=ALU.mult),
                            reads=[rstdb[ri], b_const, xb[k][h]], writes=[hb[k][h]])

        def resid_add(ps, pb, d, h):
            P.op(DVE, lambda e: e.tensor_tensor(out=xT[:, d, hs(h)], in0=ps[:, :], in1=xT[:, d, hs(h)], op=ALU.add),
                 reads=[pb, xb[d][h]], writes=[xb[d][h]])

        def proj_fm(wt, wb, col0, kstride, nk, rhs_fn, rhs_bufs, h):
            ps, pb = ps_get("A")
            P.mm([(lambda e, k=k: e.matmul(ps[:, :], wt[:, k * kstride + col0: k * kstride + col0 + 128], rhs_fn(k),
                                           start=(k == 0), stop=(k == nk - 1))) for k in range(nk)],
                 reads=[wb] + rhs_bufs, writes=[pb])
            return ps, pb

        def phase_A(li, nh, base):
            nw = nh * 4
            norm(nh, G_A + 8 * li)
            vb = reg1.new_bufs(nw * 4)
            junkb = reg1.more_bufs(1)[0]
            sbuf_ = reg2.new_bufs(16 * nw)

            def vbuf(w, s):
                return vb[w * 4 + s]

            def sbf(c, w):
                return sbuf_[c * nw + w]

            def v_item(s, w, wt, wb):
                ps, pb = ps_get("A")
                P.mm([(lambda e, k=k: e.matmul(ps[:, :], hT[:, k, w * 128:(w + 1) * 128], wt[:, k * 512:(k + 1) * 512],
                                               start=(k == 0), stop=(k == KC - 1))) for k in range(KC)],
                     reads=[wb] + [hb[k][w // 4] for k in range(KC)], writes=[pb])
                P.op(ACT, lambda e: e.activation(out=v_sb[:, w, s * 512:(s + 1) * 512], in_=ps[:, :], func=AF.Gelu),
                     reads=[pb], writes=[vbuf(w, s)])
                if s == 3:
                    P.op(ACT, lambda e: e.activation(out=junk[:, :], in_=v_sb[:, w, :], func=AF.Square,
                                                     accum_out=ssv[:, w:w + 1]),
                         reads=[vbuf(w, s_) for s_ in range(4)], writes=[junkb, ssvb])

            def rv_half(h):
                P.op(ACT, lambda e: e.activation(out=rv[:, 4 * h:4 * h + 4], in_=ssv[:, 4 * h:4 * h + 4], func=AF.Ln,
                                                 bias=epsT[:, 0:1], scale=1.0 / ADFF),
                     reads=[ssvb, b_const], writes=[rvb])
                P.op(ACT, lambda e: e.activation(out=rv[:, 4 * h:4 * h + 4], in_=rv[:, 4 * h:4 * h + 4], func=AF.Exp, scale=-0.5),
                     reads=[rvb], writes=[rvb])

            wsr_ready = {}

            def prep_wsr(w):
                wi_ = rot("wsr", 2)
                P.op(DVE, lambda e: e.tensor_scalar(out=wsr[wi_][:, :], in0=wsT[:, li, :], scalar1=rv[:, w:w + 1],
                                                     scalar2=None, op0=ALU.mult),
                     reads=[b_wsT, rvb], writes=[wsrb[wi_]])
                wsr_ready[w] = wi_

            def spatial(w):
                if w not in wsr_ready:
                    prep_wsr(w)
                wi = wsr_ready[w]
                for cq in range(4):
                    if cq == 1 and (w + 1) % 4 != 0:
                        prep_wsr(w + 1)
                    ps, pb = ps_get("B")
                    P.mm([(lambda e, cc=cc: e.matmul(ps[:, cc * 128:(cc + 1) * 128],
                                                     v_sb[:, w, (cq * 4 + cc) * 128:(cq * 4 + cc + 1) * 128],
                                                     wsr[wi][:, ((cq * 4 + cc) // 2) * 128:((cq * 4 + cc) // 2 + 1) * 128],
                                                     start=True, stop=True)) for cc in range(4)],
                         reads=[wsrb[wi]] + [vbuf(w, s_) for s_ in range(4)], writes=[pb])
                    for cc in range(4):
                        c = cq * 4 + cc
                        P.op(DVE, lambda e, c=c, cc=cc: e.scalar_tensor_tensor(
                            out=sT[:, c, w * 128:(w + 1) * 128], in0=ps[:, cc * 128:(cc + 1) * 128],
                            scalar=gains[:, G_SGU + 16 * li + c: G_SGU + 16 * li + c + 1],
                            in1=bsr[:, li, c // 2, :], op0=ALU.mult, op1=ALU.add),
                            reads=[pb, b_const], writes=[sbf(c, w)])

            def u_item_pe(s, h, cc, wt, wb):
                c = s * 4 + cc
                ps, pb = proj_fm(wt, wb, cc * 128, 512, KC, lambda k: hT[:, k, hs(h)], [hb[k][h] for k in range(KC)], h)
                ti = rot("tmp", NTMP)
                P.op(ACT, lambda e: e.activation(out=tmpf[ti][:, :], in_=ps[:, :], func=AF.Gelu), reads=[pb], writes=[tmpb[ti]])
                return (c, h, ti)

            def u_item_mul(item):
                c, h, ti = item
                wl = [sbf(c, w) for w in range(h * 4, h * 4 + 4)]
                P.op(DVE, lambda e: e.tensor_tensor(out=sT[:, c, hs(h)], in0=tmpf[ti][:, :], in1=sT[:, c, hs(h)], op=ALU.mult),
                     reads=[tmpb[ti]] + wl, writes=wl)

            def u_item(s, h, cc, wt, wb):
                u_item_mul(u_item_pe(s, h, cc, wt, wb))

            for s in range(3):
                wt, wb = w_next(base + s)
                for w in range(nw):
                    v_item(s, w, wt, wb)
                w_done()
            wt, wb = w_next(base + 3)
            for w in range(4):
                v_item(3, w, wt, wb)
            rv_half(0)
            if nh == 1:
                w_done()
                uw = w_next(base + 4)
                pend = []
                for w in range(4):
                    spatial(w)
                    pend.append(u_item_pe(0, 0, w, *uw))
                for it in pend:
                    u_item_mul(it)
                w_done()
            else:
                for w in range(4):
                    spatial(w)
                    v_item(3, 4 + w, wt, wb)
                w_done()
                rv_half(1)
                uw = w_next(base + 4)
                for w in range(4):
                    spatial(4 + w)
                    u_item(0, 0, w, *uw)
                for cc in range(4):
                    u_item(0, 1, cc, *uw)
                w_done()
            for s in range(1, 4):
                wt, wb = w_next(base + 4 + s)
                for h, cc in [(h_, c_) for h_ in range(nh) for c_ in range(4)]:
                    u_item(s, h, cc, wt, wb)
                w_done()
            for s in range(4):
                wt, wb = w_next(base + 8 + s)
                for dd in range(2):
                    d = 2 * s + dd
                    for h in range(nh):
                        rb = []
                        for c in range(16):
                            rb += [sbf(c, w) for w in range(h * 4, h * 4 + 4)]
                        ps, pb = proj_fm(wt, wb, dd * 128, 256, 16, lambda j: sT[:, j, hs(h)], rb, h)
                        resid_add(ps, pb, d, h)
                w_done()

        def phase_F(layer, nh, base):
            norm(nh, G_F + 8 * layer)
            ab = reg1.new_bufs(FJ * nh)

            def abf(j, h):
                return ab[j * nh + h]

            for s in range(11):
                wt, wb = w_next(base + s)
                for h, jj in [(h_, j_) for h_ in range(nh) for j_ in range(2)]:
                    j = 2 * s + jj
                    if True:
                        if 'nogu' in KDBG:
                            continue
                        rbufs = [hb[k][h] for k in range(KC)]
                        psg, pbg = proj_fm(wt, wb, jj * 128, 512, KC, lambda k: hT[:, k, hs(h)], rbufs, h)
                        psu, pbu = proj_fm(wt, wb, 256 + jj * 128, 512, KC, lambda k: hT[:, k, hs(h)], rbufs, h)
                        ti = rot("tmp", NTMP)
                        P.op(ACT, lambda e: e.activation(out=tmpf[ti][:, :], in_=psg[:, :], func=AF.Silu),
                             reads=[pbg], writes=[tmpb[ti]])
                        P.op(DVE, lambda e: e.tensor_tensor(out=actT[:, j, hs(h)], in0=psu[:, :], in1=tmpf[ti][:, :],
                                                             op=ALU.mult), reads=[pbu, tmpb[ti]], writes=[abf(j, h)])
                w_done()
            for d in range(8):
                wt, wb = w_next(base + 11 + d)
                for h in range(nh):
                    if 'nodown' in KDBG:
                        continue
                    ps, pb = proj_fm(wt, wb, 0, 128, FJ, lambda j: actT[:, j, hs(h)], [abf(j, h) for j in range(FJ)], h)
                    resid_add(ps, pb, d, h)
                w_done()

        def phase_KV(tok0, nh, is_halo, base, after_norm=None):
            nb = nh * 4
            norm(nh, G_KV)
            if after_norm is not None:
                after_norm()
            kb_ = reg2.new_bufs(KC * nh)
            vsb_ = reg2.more_bufs(nb * 2 + 1)
            onesb = vsb_[-1]
            P.op(DVE, lambda e: e.memset(vst5[:, 0:nb, :, 1, :], 0.0), writes=[onesb])
            for s in range(2):
                wt, wb = w_next(base + s)
                for h, cc in [(h_, c_) for h_ in range(nh) for c_ in range(4)]:
                    c = 4 * s + cc
                    if True:
                        ps, pb = proj_fm(wt, wb, cc * 128, 512, KC, lambda k: hT[:, k, hs(h)],
                                         [hb[k][h] for k in range(KC)], h)
                        P.op(ACT, lambda e: e.activation(out=kt_sb[:, c, hs(h)], in_=ps[:, :], func=AF.Copy),
                             reads=[pb], writes=[kb_[c * nh + h]])
                w_done()
            for s in range(2):
                wt, wb = w_next(base + 2 + s)
                for blk in range(nb):
                    ps, pb = ps_get("A")
                    P.mm([(lambda e, k=k: e.matmul(ps[:, :], hT[:, k, blk * 128:(blk + 1) * 128], wt[:, k * 512:(k + 1) * 512],
                                                   start=(k == 0), stop=(k == KC - 1))) for k in range(KC)],
                         reads=[wb] + [hb[k][blk // 4] for k in range(KC)], writes=[pb])
                    ps4 = ps[:, :].rearrange("p (q s d) -> p q s d", q=4, s=2)
                    P.op(DVE, lambda e: e.tensor_copy(out=vst5[:, blk, 4 * s:4 * s + 4, 0, :], in_=ps4[:, :, 0, :]),
                         reads=[pb], writes=[vsb_[blk * 2 + s]])
                    P.op(ACT, lambda e: e.activation(out=vst5[:, blk, 4 * s:4 * s + 4, 2, :], in_=ps4[:, :, 1, :],
                                                     func=AF.Copy), reads=[pb], writes=[vsb_[blk * 2 + s]])
                w_done()
            for h in range(nh):
                bi = tok0 // 512 + h
                t0 = tok0 + h * 512
                P.dma(SP, kT_d[:, :, t0:t0 + 512], kt_sb[:, :, hs(h)],
                      reads=[kb_[c * nh + h] for c in range(KC)], writes=[kdb[bi]])
                P.dma(SP, v_d[:, t0 // 128: t0 // 128 + 4, :], vst[:, h * 4:(h + 1) * 4, :],
                      reads=[vsb_[blk * 2 + s] for blk in range(h * 4, h * 4 + 4) for s in range(2)] + [onesb],
                      writes=[vdb[bi]])

        def phase_B(li, tok0, nh, base):
            norm(nh, G_B + 8 * li)
            qb = reg2.new_bufs(KC * nh)
            qb2 = reg2.more_bufs(KC * nh)
            ob = reg2.more_bufs(KC * nh)
            for s in range(2):
                wt, wb = w_next(base + s)
                for h, cc in [(h_, c_) for h_ in range(nh) for c_ in range(4)]:
                    c = 4 * s + cc
                    if True:
                        ps, pb = proj_fm(wt, wb, cc * 128, 512, KC, lambda k: hT[:, k, hs(h)],
                                         [hb[k][h] for k in range(KC)], h)
                        P.op(ACT, lambda e: e.activation(out=QTa[:, c, hs(h)], in_=ps[:, :], func=AF.Copy, scale=0.125),
                             reads=[pb], writes=[qb[c * nh + h]])
                        P.op(ACT, lambda e: e.activation(out=QTb[:, c, hs(h)], in_=ps[:, :], func=AF.Copy, scale=0.125),
                             reads=[pb], writes=[qb2[c * nh + h]])
                        P.op(DVE, lambda e: e.memset(QTa[64:128, c, hs(h)], 0.0), writes=[qb[c * nh + h]])
                        P.op(DVE, lambda e: e.memset(QTb[0:64, c, hs(h)], 0.0), writes=[qb2[c * nh + h]])
                w_done()
            kws = reg1.new_bufs(2)
            vws = reg1.more_bufs(2)
            for h in range(nh):
                bi = tok0 // 512 + h
                t0 = tok0 + h * 512
                sp_, so_ = h % 2, 1 - h % 2
                if h == 0:
                    P.dma(SP, KTw[:, :, sp_ * 512:(sp_ + 1) * 512], kT_d[:, :, t0 - 512:t0], reads=[kdb[bi - 1]], writes=[kws[sp_]])
                    P.dma(SP, Vw[:, sp_ * 4:(sp_ + 1) * 4, :], v_d[:, (t0 - 512) // 128:t0 // 128, :], reads=[vdb[bi - 1]],
                          writes=[vws[sp_]])
                P.dma(SP, KTw[:, :, so_ * 512:(so_ + 1) * 512], kT_d[:, :, t0:t0 + 512], reads=[kdb[bi]], writes=[kws[so_]])
                P.dma(SP, Vw[:, so_ * 4:(so_ + 1) * 4, :], v_d[:, t0 // 128:(t0 + 512) // 128, :], reads=[vdb[bi]],
                      writes=[vws[so_]])

                def kslot(kb):
                    return (sp_ if kb < 4 else so_)

                def kblk(kb):
                    return kslot(kb) * 4 + kb % 4
                steps = [(hd, kb) for hd in range(NHEAD) for kb in range(8)]
                st = {}
                LA = 3

                def emit_S(i):
                    hd, kb = steps[i]
                    c, po = hd // 2, (hd % 2) * 64
                    if kb == 0:
                        bi_ = rot("bh", NBH)
                        P.dma(SP, bht[bi_][:, :], bias_d[li, hd], writes=[bhb[bi_]])
                        st[("bh", hd)] = bi_
                    bi_ = st[("bh", hd)]
                    i_lo, i_hi = max(0, 2 * kb - 8), min(7, 2 * kb + 1)
                    nq = i_hi - i_lo + 1
                    m_lo = 8 + i_lo - 2 * kb
                    ps, pb = ps_get("A")
                    qt_, qtb_ = (QTa, qb) if hd % 2 == 0 else (QTb, qb2)
                    P.mm([lambda e: e.matmul(ps[:, 0:nq * 64], KTw[:, c, kblk(kb) * 128:(kblk(kb) + 1) * 128],
                                             qt_[:, c, h * 512 + i_lo * 64: h * 512 + (i_hi + 1) * 64],
                                             start=True, stop=True)],
                         reads=[kws[kslot(kb)], qtb_[c * nh + h]], writes=[pb])
                    ti = rot("tmp", NTMP)
                    P.op(DVE, lambda e: e.tensor_tensor(out=tmpf[ti][:, 0:nq * 64], in0=ps[:, 0:nq * 64],
                                                         in1=bht[bi_][:, m_lo * 64:(m_lo + nq) * 64], op=ALU.add),
                         reads=[pb, bhb[bi_]], writes=[tmpb[ti]])
                    pi = rot("pt", NPT)
                    P.op(ACT, lambda e: e.activation(out=ptb[pi][:, 0:nq * 64], in_=tmpf[ti][:, 0:nq * 64], func=AF.Exp),
                         reads=[tmpb[ti]], writes=[ptbb[pi]])
                    st[("pt", i)] = (pi, i_lo)

                def emit_PV(i):
                    hd, kb = steps[i]
                    c, po = hd // 2, (hd % 2) * 64
                    pi, i_lo = st.pop(("pt", i))
                    i_hi = min(7, 2 * kb + 1)
                    if kb == 0 and hd % 2 == 0:
                        st[("o", c)] = (ps_get("B"), ps_get("B"))
                    (nps, npb), (dps, dpb) = st[("o", c)]
                    pt = ptb[pi]
                    vc0 = c * 192 + (hd % 2) * 64
                    halo_keys = (bi - 1 == 0) and kb < 4
                    oc = (2 if halo_keys else 0) + (hd % 2)
                    first = (kb == 0 and hd % 2 == 0)
                    last = (kb == 7 and hd % 2 == 1)

                    def col(iq):
                        return (iq - i_lo) * 64

                    segs = [(i_lo, i_hi + 1, 0, 128)]
                    fns = []
                    if first:
                        fns.append(lambda e: e.matmul(nps[:, :], onesLR[:, 4, :], QTa[:, c, hs(h)], start=True, stop=False))
                        fns.append(lambda e: e.matmul(dps[:, :], onesLR[:, 4, :], QTa[:, c, hs(h)], start=True, stop=False))
                    for si, (c0, c1, p0, p1) in enumerate(segs):
                        fns.append(lambda e, c0=c0, c1=c1, p0=p0, p1=p1, si=si: e.matmul(
                            nps[:, c0 * 64:c1 * 64], Vw[p0:p1, kblk(kb), vc0:vc0 + 128], pt[p0:p1, col(c0):col(c1)],
                            start=False, stop=(last and si == len(segs) - 1)))
                    for si, (c0, c1, p0, p1) in enumerate(segs):
                        fns.append(lambda e, c0=c0, c1=c1, p0=p0, p1=p1, si=si: e.matmul(
                            dps[:, c0 * 64:c1 * 64], onesLR[p0:p1, oc, :], pt[p0:p1, col(c0):col(c1)],
                            start=False, stop=(last and si == len(segs) - 1)))
                    if first:
                        P.mm(fns, reads=[ptbb[pi], vws[kslot(kb)], b_const, qb[c * nh + h]], writes=[npb, dpb])
                    else:
                        P.mm(fns, reads=[ptbb[pi], vws[kslot(kb)], b_const], acc=[npb, dpb])
                    if last and 'nonorm' not in KDBG:
                        ti = rot("tmp", NTMP)
                        P.op(ACT, lambda e: e.activation(out=tmpf[ti][:, :], in_=dps[:, :], func=AF.Ln), reads=[dpb], writes=[tmpb[ti]])
                        P.op(ACT, lambda e: e.activation(out=tmpf[ti][:, :], in_=tmpf[ti][:, :], func=AF.Exp, scale=-1.0),
                             reads=[tmpb[ti]], writes=[tmpb[ti]])
                        P.op(DVE, lambda e: e.tensor_tensor(out=OT[:, c, hs(h)], in0=nps[:, :], in1=tmpf[ti][:, :], op=ALU.mult),
                             reads=[npb, tmpb[ti]], writes=[ob[c * nh + h]])
                        del st[("o", c)]

                n = len(steps)
                if 'noattn' in KDBG:
                    n = -LA
                for i in range(n + LA):
                    if i < n:
                        emit_S(i)
                    if i - LA >= 0 and 'nopv' not in KDBG:
                        emit_PV(i - LA)
            for s in range(2):
                wt, wb = w_next(base + 2 + s)
                for dd in range(4):
                    d = 4 * s + dd
                    for h in range(nh):
                        ps, pb = proj_fm(wt, wb, dd * 128, 512, KC, lambda c: OT[:, c, hs(h)],
                                         [ob[c * nh + h] for c in range(KC)], h)
                        resid_add(ps, pb, d, h)
                w_done()

        prefetched = False
        for (tok0, T, is_halo) in tiles:
            nh = T // 512
            if not (prefetched and tok0 == HALO):
                for h in range(nh):
                    P.dma(SP, xT[:, :, hs(h)], x_d[:, :, tok0 + h * 512: tok0 + (h + 1) * 512],
                          writes=[xb[k][h] for k in range(KC)])
            ph = 0
            base = 0
            for li in range(2):
                if nphases > ph:
                    phase_A(li, nh, base)
                ph += 1
                if nphases > ph:
                    phase_F(li, nh, base + 12)
                ph += 1
                base += SL_A
            if nphases > ph:
                def prefetch_next_x():
                    for h_ in range(2):
                        P.dma(ACT, xT[:, :, hs(h_)], x_d[:, :, HALO + h_ * 512: HALO + (h_ + 1) * 512],
                              writes=[xb[k][h_] for k in range(KC)])
                if is_halo:
                    prefetched = True
                phase_KV(tok0, nh, is_halo, base, after_norm=prefetch_next_x if is_halo else None)
            ph += 1
            base += SL_KV
            if is_halo:
                continue
            for li in range(2):
                if nphases > ph:
                    phase_B(li, tok0, nh, base)
                ph += 1
                if nphases > ph:
                    phase_F(2 + li, nh, base + 4)
                ph += 1
                base += SL_B
            if nphases > ph:
                norm(nh, G_FIN, final=True)
            for h in range(nh):
                o0 = tok0 - HALO + h * 512
                P.dma(SP, out_d[:, :, o0:o0 + 512], xT[:, :, hs(h)], reads=[xb[k][h] for k in range(KC)])
        assert wstate["used"] == len(wseq), (wstate, len(wseq))
        for sem, val in SP.dsem:
            if val > 0:
                nc.sync.wait_ge(sem, val)
    return nc


_CACHE = {}


def _prep(inputs):
    inp = {k: np.asarray(v, dtype=np.float32) for k, v in inputs.items()}
    wstream = _build_wstream(inp)
    gains, bsr, wsT, bias = _build_small(inp)
    x = inp["x"]
    in_maps = []
    for c in range(NCORE):
        b, half = c // 2, c % 2
        xs = np.zeros((NTOK, D), np.float32)
        if half == 1:
            xs[:] = x[b, OWN - HALO: SEQ]
        else:
            xs[HALO:] = x[b, 0:OWN]
        x_t = np.ascontiguousarray(xs.reshape(NTOK, KC, 128).transpose(2, 1, 0))
        flag = np.full((128, 1), float(half), np.float32)
        in_maps.append({"x_t": x_t, "wstream": wstream, "gains": gains, "bsr": bsr, "wsT": wsT,
                        "flag": flag, "relb": bias})
    return in_maps


def _run(inputs, nphases=10, trace=False):
    in_maps = _prep(inputs)
    nc = build(nphases)
    kw = {"trace": True} if trace else {}
    res = run_bass_kernel_spmd(nc, in_maps, core_ids=list(range(NCORE)), **kw)
    out = np.zeros((BATCH, SEQ, D), np.float32)
    for c in range(NCORE):
        b, half = c // 2, c % 2
        o = np.asarray(res.results[c]["out_t"])
        out[b, half * OWN:(half + 1) * OWN] = o.transpose(2, 1, 0).reshape(OWN, D)
    return out, res


def kernel(**inputs):
    out, _ = _run(inputs)
    return out
```
